# Optimizing a Trainium2 kernel written in Bass

```python
import jax
import jax.numpy as jnp
from jax import lax
import numpy as np

D_MODEL = 1024
BATCH = 4
SEQ = 8192
DEPTH = 2

GRID_W = 64
CTX_LEN = 256
N_MIXERS = 2
N_ATT = (DEPTH + 1) // 2
N_RWKV = DEPTH // 2
HEAD_DIM = 64
ATT_Q_HEADS = D_MODEL // HEAD_DIM
ATT_KV_HEADS = ATT_Q_HEADS // 4
ATT_GROUP = ATT_Q_HEADS // ATT_KV_HEADS
WINDOW = 128
BLOCK = 128
ROPE_BASE = 10000.0
RWKV_HEADS = D_MODEL // HEAD_DIM
DECAY_LORA = 64
ICLR_LORA = 64
GATE_LORA = 128
D_FF = 4 * D_MODEL
NORM_EPS = 1e-6
GN_EPS = 64e-5
NEG_INF = -1e30
F32 = jnp.float32

kernel_name = 'hybrid_swa_rwkv7_dit_layers'


def rmsnorm(x, g):
    x32 = x.astype(F32)
    y = x32 * lax.rsqrt(jnp.mean(x32 * x32, axis=-1, keepdims=True) + NORM_EPS)
    return (y * g.astype(F32)).astype(x.dtype)


def rope_1d(t, pos):
    half = t.shape[-1] // 2
    freqs = ROPE_BASE ** (-jnp.arange(half, dtype=F32) / half)
    ang = pos.astype(F32)[:, None, None] * freqs
    cos, sin = jnp.cos(ang), jnp.sin(ang)
    t1, t2 = t[..., :half].astype(F32), t[..., half:].astype(F32)
    return jnp.concatenate([t1 * cos - t2 * sin, t1 * sin + t2 * cos], axis=-1).astype(t.dtype)


def rope_2d(t, rows, cols):
    r = HEAD_DIM // 2
    return jnp.concatenate([rope_1d(t[..., :r], rows), rope_1d(t[..., r:], cols)], axis=-1)


def sq_relu_mlp(h, w1, w2):
    return jnp.square(jax.nn.relu(h @ w1)) @ w2


def attention_mixer(h, hc, w_qkv, q_gain, k_gain, sink, w_o, rows, cols, ctx_out):
    B, S, _ = h.shape
    L = hc.shape[1]
    nq, nk = ATT_Q_HEADS * HEAD_DIM, ATT_KV_HEADS * HEAD_DIM

    def project(t):
        T = t.shape[1]
        qkv = t @ w_qkv
        q = rmsnorm(qkv[..., :nq].reshape(B, T, ATT_Q_HEADS, HEAD_DIM), q_gain)
        k = rmsnorm(qkv[..., nq:nq + nk].reshape(B, T, ATT_KV_HEADS, HEAD_DIM), k_gain)
        v = qkv[..., nq + nk:].reshape(B, T, ATT_KV_HEADS, HEAD_DIM)
        return q, k, v

    q, k, v = project(h)
    q, k = rope_2d(q, rows, cols), rope_2d(k, rows, cols)
    qc, kc, vc = project(hc)
    scale = HEAD_DIM ** -0.5
    sink_logit = sink.astype(F32).reshape(ATT_KV_HEADS, ATT_GROUP, 1, 1)

    nb = S // BLOCK
    qb = q.reshape(B, nb, BLOCK, ATT_KV_HEADS, ATT_GROUP, HEAD_DIM)

    def band(t):
        tp = jnp.pad(t, ((0, 0), (BLOCK, BLOCK), (0, 0), (0, 0))).reshape(B, nb + 2, BLOCK, ATT_KV_HEADS, HEAD_DIM)
        return jnp.concatenate([tp[:, :-2], tp[:, 1:-1], tp[:, 2:]], axis=2)

    kw, vw = band(k), band(v)
    s_win = jnp.einsum('bnqhgd,bnkhd->bnhgqk', qb, kw).astype(F32) * scale
    s_ctx = jnp.einsum('bnqhgd,blhd->bnhgql', qb, kc).astype(F32) * scale
    blk = jnp.arange(nb)[:, None, None] * BLOCK
    qpos = blk + jnp.arange(BLOCK)[None, :, None]
    kpos = blk - BLOCK + jnp.arange(3 * BLOCK)[None, None, :]
    mask = (jnp.abs(kpos - qpos) <= WINDOW) & (kpos >= 0) & (kpos < S)
    s_win = jnp.where(mask[None, :, None, None], s_win, NEG_INF)
    sinks = jnp.broadcast_to(sink_logit, s_win.shape[:-1] + (1,))
    p = jax.nn.softmax(jnp.concatenate([s_win, s_ctx, sinks], axis=-1), axis=-1)
    p_win = p[..., :3 * BLOCK].astype(v.dtype)
    p_ctx = p[..., 3 * BLOCK:3 * BLOCK + L].astype(v.dtype)
    o = jnp.einsum('bnhgqk,bnkhd->bnqhgd', p_win, vw) + jnp.einsum('bnhgql,blhd->bnqhgd', p_ctx, vc)
    y = o.reshape(B, S, D_MODEL) @ w_o
    if not ctx_out:
        return y, None

    qcg = qc.reshape(B, L, ATT_KV_HEADS, ATT_GROUP, HEAD_DIM)
    s_cc = jnp.einsum('blhgd,bmhd->bhglm', qcg, kc).astype(F32) * scale
    sinks_c = jnp.broadcast_to(sink_logit, s_cc.shape[:-1] + (1,))
    pc = jax.nn.softmax(jnp.concatenate([s_cc, sinks_c], axis=-1), axis=-1)[..., :L].astype(vc.dtype)
    yc = jnp.einsum('bhglm,bmhd->blhgd', pc, vc).reshape(B, L, D_MODEL) @ w_o
    return y, yc


def wkv_scan(r, w, k, v, a, b, s0, reverse):
    def step(s, inp):
        r_t, w_t, k_t, v_t, a_t, b_t = inp
        sa = jnp.einsum('bhvk,bhk->bhv', s, a_t)
        s = s * w_t[:, :, None, :] + sa[..., None] * b_t[:, :, None, :] + v_t[..., None] * k_t[:, :, None, :]
        return s, jnp.einsum('bhvk,bhk->bhv', s, r_t)

    xs = tuple(jnp.moveaxis(t, 1, 0) for t in (r, w, k, v, a, b))
    s_final, ys = lax.scan(step, s0, xs, reverse=reverse)
    return s_final, jnp.moveaxis(ys, 0, 1)


def rwkv_features(h, mu, w_rkv, w0, w1, w2, a0, a1, a2, k_k, k_a):
    B, T, _ = h.shape
    hn = (RWKV_HEADS, HEAD_DIM)
    hp = jnp.pad(h, ((0, 0), (1, 1), (0, 0)))
    xx = 0.5 * (hp[:, :-2] + hp[:, 2:]) - h
    xr, xw, xk, xv, xa, xg = [h + xx * mu[m] for m in range(6)]

    def heads(t):
        return t.astype(F32).reshape(B, T, *hn)

    r = heads(xr @ w_rkv[0])
    k = heads(xk @ w_rkv[1])
    v = heads(xv @ w_rkv[2])
    decay = tuple(
        heads(jnp.exp(-jnp.exp(-jax.nn.softplus(-(w0[d] + jnp.tanh(xw @ w1[d]) @ w2[d]).astype(F32)) - 0.5)))
        for d in range(2))
    a = heads(jax.nn.sigmoid((a0 + (xa @ a1) @ a2).astype(F32)))
    kk = k * k_k.astype(F32).reshape(hn)
    kk = kk / jnp.maximum(jnp.sqrt(jnp.sum(kk * kk, axis=-1, keepdims=True)), 1e-12)
    k = k * (1.0 + (a - 1.0) * k_a.astype(F32).reshape(hn))
    return r, k, v, -kk, kk * a, decay, xg


def group_norm(y, g, b):
    hn = (RWKV_HEADS, HEAD_DIM)
    mean = jnp.mean(y, axis=-1, keepdims=True)
    var = jnp.mean(jnp.square(y - mean), axis=-1, keepdims=True)
    return (y - mean) * lax.rsqrt(var + GN_EPS) * g.astype(F32).reshape(hn) + b.astype(F32).reshape(hn)


def rwkv_readout(y_fwd, y_bwd, r, k, v, xg, g1, g2, r_k, gn_g, gn_b, w_o):
    B, T = xg.shape[:2]
    bonus = jnp.sum(r * k * r_k.astype(F32), axis=-1, keepdims=True) * v

    def gate(d):
        return (jax.nn.sigmoid(xg @ g1[d]) @ g2[d]).astype(F32).reshape(B, T, RWKV_HEADS, HEAD_DIM)

    o = (group_norm(y_fwd, gn_g, gn_b) + bonus) * gate(0) + (group_norm(y_bwd, gn_g, gn_b) + bonus) * gate(1)
    return o.reshape(B, T, D_MODEL).astype(xg.dtype) @ w_o


def rwkv_mixer(h, hc, mu, w_rkv, w0, w1, w2, a0, a1, a2, g1, g2, k_k, k_a, r_k, gn_g, gn_b, w_o, ctx_out):
    feat = (mu, w_rkv, w0, w1, w2, a0, a1, a2, k_k, k_a)
    rc, kc, vc, nac, bc, decc, xgc = rwkv_features(hc, *feat)
    r, k, v, na, b, dec, xg = rwkv_features(h, *feat)
    zero = jnp.zeros((h.shape[0], RWKV_HEADS, HEAD_DIM, HEAD_DIM), F32)
    s_fwd, yc_fwd = wkv_scan(rc, decc[0], kc, vc, nac, bc, zero, False)
    s_bwd, yc_bwd = wkv_scan(rc, decc[1], kc, vc, nac, bc, zero, True)
    _, y_fwd = wkv_scan(r, dec[0], k, v, na, b, s_fwd, False)
    _, y_bwd = wkv_scan(r, dec[1], k, v, na, b, s_bwd, True)
    ro = (g1, g2, r_k, gn_g, gn_b, w_o)
    y = rwkv_readout(y_fwd, y_bwd, r, k, v, xg, *ro)
    if not ctx_out:
        return y, None
    return y, rwkv_readout(yc_fwd, yc_bwd, rc, kc, vc, xgc, *ro)


def setup_inputs(seed: int = 0) -> dict:
    key = jax.random.key(seed)
    ks = iter(jax.random.split(key, 40))
    D = D_MODEL

    def nrm(shape, s):
        return jax.random.normal(next(ks), shape, F32) * s

    def unif(shape, lo, hi):
        return jax.random.uniform(next(ks), shape, F32, lo, hi)

    qkv_cols = (ATT_Q_HEADS + 2 * ATT_KV_HEADS) * HEAD_DIM
    return {
        'x': nrm((BATCH, SEQ, D), 1.0),
        'c': nrm((BATCH, D), 1.0),
        'ctx': nrm((BATCH, CTX_LEN, D), 1.0),
        'c_ctx': nrm((D,), 1.0),
        'ada_w': nrm((DEPTH, D, 6 * D), 0.5 * D ** -0.5),
        'ada_b': nrm((DEPTH, 6 * D), 0.01),
        'norm1_g': 1.0 + nrm((DEPTH, D), 0.1),
        'norm2_g': 1.0 + nrm((DEPTH, D), 0.1),
        'mlp_w1': nrm((DEPTH, D, D_FF), D ** -0.5),
        'mlp_w2': nrm((DEPTH, D_FF, D), D_FF ** -0.5),
        'att_w_qkv': nrm((N_ATT, D, qkv_cols), D ** -0.5),
        'att_q_gain': 1.0 + nrm((N_ATT, HEAD_DIM), 0.1),
        'att_k_gain': 1.0 + nrm((N_ATT, HEAD_DIM), 0.1),
        'att_sink': nrm((N_ATT, ATT_Q_HEADS), 1.0),
        'att_w_o': nrm((N_ATT, D, D), D ** -0.5),
        'rwkv_mu': unif((N_RWKV, 6, D), 0.0, 1.0),
        'rwkv_w_rkv': nrm((N_RWKV, 3, D, D), D ** -0.5),
        'rwkv_w0': unif((N_RWKV, 2, D), -6.0, 1.0),
        'rwkv_w1': nrm((N_RWKV, 2, D, DECAY_LORA), D ** -0.5),
        'rwkv_w2': nrm((N_RWKV, 2, DECAY_LORA, D), 0.5 * DECAY_LORA ** -0.5),
        'rwkv_a0': nrm((N_RWKV, D), 0.5),
        'rwkv_a1': nrm((N_RWKV, D, ICLR_LORA), D ** -0.5),
        'rwkv_a2': nrm((N_RWKV, ICLR_LORA, D), ICLR_LORA ** -0.5),
        'rwkv_g1': nrm((N_RWKV, 2, D, GATE_LORA), D ** -0.5),
        'rwkv_g2': nrm((N_RWKV, 2, GATE_LORA, D), GATE_LORA ** -0.5),
        'rwkv_k_k': 0.85 + nrm((N_RWKV, D), 0.05),
        'rwkv_k_a': 1.0 + nrm((N_RWKV, D), 0.05),
        'rwkv_r_k': nrm((N_RWKV, RWKV_HEADS, HEAD_DIM), 0.1),
        'rwkv_gn_g': 1.0 + nrm((N_RWKV, D), 0.1),
        'rwkv_gn_b': nrm((N_RWKV, D), 0.01),
        'rwkv_w_o': nrm((N_RWKV, D, D), D ** -0.5),
    }


def reference(x, c, ctx, c_ctx, ada_w, ada_b, norm1_g, norm2_g, mlp_w1, mlp_w2,
              att_w_qkv, att_q_gain, att_k_gain, att_sink, att_w_o,
              rwkv_mu, rwkv_w_rkv, rwkv_w0, rwkv_w1, rwkv_w2, rwkv_a0, rwkv_a1, rwkv_a2,
              rwkv_g1, rwkv_g2, rwkv_k_k, rwkv_k_a, rwkv_r_k, rwkv_gn_g, rwkv_gn_b, rwkv_w_o):
    S = x.shape[1]
    n_rows = S // GRID_W
    rows = jnp.repeat(jnp.arange(n_rows), GRID_W)
    cols = jnp.tile(jnp.arange(GRID_W), n_rows)
    silu_c = jax.nn.silu(c)[:, None, :]
    silu_cc = jax.nn.silu(c_ctx)[None, None, :]
    xc = ctx
    for i in range(DEPTH):
        ctx_out = i < DEPTH - 1
        j = i // N_MIXERS
        sh1, sc1, gt1, sh2, sc2, gt2 = jnp.split(silu_c @ ada_w[i] + ada_b[i], 6, axis=-1)
        csh1, csc1, cgt1, csh2, csc2, cgt2 = jnp.split(silu_cc @ ada_w[i] + ada_b[i], 6, axis=-1)
        h = rmsnorm(x, norm1_g[i]) * (1.0 + sc1) + sh1
        hc = rmsnorm(xc, norm1_g[i]) * (1.0 + csc1) + csh1
        if i % N_MIXERS == 0:
            y, yc = attention_mixer(h, hc, att_w_qkv[j], att_q_gain[j], att_k_gain[j], att_sink[j],
                                    att_w_o[j], rows, cols, ctx_out)
        else:
            y, yc = rwkv_mixer(h, hc, rwkv_mu[j], rwkv_w_rkv[j], rwkv_w0[j], rwkv_w1[j], rwkv_w2[j],
                               rwkv_a0[j], rwkv_a1[j], rwkv_a2[j], rwkv_g1[j], rwkv_g2[j],
                               rwkv_k_k[j], rwkv_k_a[j], rwkv_r_k[j], rwkv_gn_g[j], rwkv_gn_b[j],
                               rwkv_w_o[j], ctx_out)
        x = x + gt1 * y
        x = x + gt2 * sq_relu_mlp(rmsnorm(x, norm2_g[i]) * (1.0 + sc2) + sh2, mlp_w1[i], mlp_w2[i])
        if ctx_out:
            xc = xc + cgt1 * yc
            xc = xc + cgt2 * sq_relu_mlp(rmsnorm(xc, norm2_g[i]) * (1.0 + csc2) + csh2, mlp_w1[i], mlp_w2[i])
    return x
```

```python
import contextlib
import numpy as np
import concourse.bass as bass
import concourse.mybir as mybir
from concourse.bass_utils import run_bass_kernel_spmd

F32 = mybir.dt.float32
F32R = mybir.dt.float32r
AF = mybir.ActivationFunctionType
ALU = mybir.AluOpType

ENGS = ('pe', 'act', 'dve', 'pool', 'sp')
SEM_ROLL = 8000
D = 1024
KC = 8
NORM_EPS = 1e-6
GN_EPS = 64e-5


class Trk:
    __slots__ = ('w', 'r')

    def __init__(self):
        self.w = None
        self.r = {}


class Prog:
    def __init__(self, nc, n_dma_sems=16):
        self.nc = nc
        self.es = contextlib.ExitStack()
        self.q = {e: [] for e in ENGS}
        self.cnt = {e: 0 for e in ENGS}
        self.semgen = {e: 0 for e in ENGS}
        self.sems = {}
        self.seen = {e: {} for e in ENGS}
        for e in ENGS:
            if e != 'sp':
                self._newsem(e)
        self.dma_sems = [self.es.enter_context(nc.semaphore('dma%d' % i)) for i in range(n_dma_sems)]
        self.dma_val = [0] * n_dma_sems
        self.dma_rr = 0
        self.ninst = {e: 0 for e in ENGS}
        self.psb = []
        self.ps_rr = 0
        self.sw = {}

    def _newsem(self, e):
        self.semgen[e] += 1
        key = (e, self.semgen[e])
        self.sems[key] = self.es.enter_context(self.nc.semaphore('%s_%d' % key))
        self.cnt[e] = 0
        return key

    def sbuf(self, name, shape, dt=F32):
        return self.es.enter_context(self.nc.sbuf_tensor(name, list(shape), dt))

    def psum(self, name, shape, dt=F32):
        return self.es.enter_context(self.nc.psum_tensor(name, list(shape), dt))

    def init_psum(self, n=8):
        for i in range(n):
            self.psb.append((self.psum('psb%d' % i, [128, 512]), Trk()))

    def next_ps(self):
        r = self.psb[self.ps_rr]
        self.ps_rr = (self.ps_rr + 1) % len(self.psb)
        return r

    def _emit_waits(self, e, deps):
        seen = self.seen[e]
        need = {}
        for (k, v) in deps:
            if k[0] == e and e == 'pe':
                continue
            if seen.get(k, 0) >= v:
                continue
            if need.get(k, 0) < v:
                need[k] = v
        for k, v in need.items():
            seen[k] = v
            sem = self.sems[k] if k[0] != 'dma' else self.dma_sems[k[1]]
            self.q[e].append(('w', sem, v))

    @staticmethod
    def _deps(r, w):
        deps = []
        for t in r:
            if t.w is not None:
                deps.append(t.w)
        for t in w:
            if t.w is not None:
                deps.append(t.w)
            deps.extend(t.r.items())
        return deps

    def op(self, e, name, *args, r=(), w=(), **kw):
        fn = (name, args, kw)
        self._emit_waits(e, self._deps(r, w))
        if self.cnt[e] >= SEM_ROLL:
            self._newsem(e)
        self.cnt[e] += 1
        key = (e, self.semgen[e])
        val = self.cnt[e]
        self.q[e].append(('i', fn, self.sems[key], 1))
        self.ninst[e] += 1
        for t in r:
            if t.r.get(key, 0) < val:
                t.r[key] = val
        for t in w:
            t.w = (key, val)
            t.r = {}

    def swdma(self, out, in_, w):
        e = 'pool'
        owner = id(w[0])
        if owner not in self.sw:
            self.sw[owner] = [self.es.enter_context(self.nc.semaphore('sw%d' % len(self.sw))), 0, w[0]]
        ent = self.sw[owner]
        ent[1] += 1
        key = ('sw', owner, ent[1])
        self.sems[key] = ent[0]
        self._emit_waits(e, self._deps((), w))
        if ent[1] > 1:
            self.q[e].append(('c', ent[0]))
        self.q[e].append(('i', ('dma_start', (), dict(out=out, in_=in_)), ent[0], 16))
        self.ninst[e] += 1
        for t in w:
            t.w = (key, 16)
            t.r = {}

    def dma(self, e, out, in_, r=(), w=()):
        if e == 'pool':
            return self.swdma(out, in_, w)
        i = self.dma_rr
        self.dma_rr = (self.dma_rr + 1) % len(self.dma_sems)
        key = ('dma', i)
        deps = self._deps(r, w)
        if self.dma_val[i] > 0:
            deps.append((key, self.dma_val[i]))
        self._emit_waits(e, deps)
        self.dma_val[i] += 16
        val = self.dma_val[i]
        self.q[e].append(('i', ('dma_start', (), dict(out=out, in_=in_)), self.dma_sems[i], 16))
        self.ninst[e] += 1
        for t in r:
            t.r[key] = val
        for t in w:
            t.w = (key, val)
            t.r = {}

    def wait_all(self, e, trks):
        self._emit_waits(e, [t.w for t in trks if t.w is not None])

    def finish(self):
        q = self.q

        def run(eng, lst):
            for it in lst:
                if it[0] == 'w':
                    eng.wait_ge(it[1], it[2])
                elif it[0] == 'c':
                    eng.sem_clear(it[1])
                else:
                    nm, a, kw = it[1]
                    getattr(eng, nm)(*a, **kw).then_inc(it[2], it[3])

        with self.nc.Block() as block:
            @block.sync
            def _(eng):
                run(eng, q['sp'])

            @block.tensor
            def _(eng):
                run(eng, q['pe'])

            @block.scalar
            def _(eng):
                run(eng, q['act'])

            @block.vector
            def _(eng):
                run(eng, q['dve'])

            @block.gpsimd
            def _(eng):
                run(eng, q['pool'])
        self.es.close()


class Buf:
    def __init__(self, P, name, shape, nk=1, dt=F32):
        self.t = P.sbuf(name, shape, dt)
        self.ks = [Trk() for _ in range(nk)]

    @property
    def k(self):
        return self.ks[0]


def R(ap):
    return ap.bitcast(F32R)


class Ctx:
    def __init__(self, nc):
        self.nc = nc
        self.P = Prog(nc)
        self.P.init_psum(8)
        P = self.P
        self.wb = [Buf(P, 'wb%d' % i, [128, 4096]) for i in range(2)]
        self.wb_rr = 0
        self.tmp = {}
        self.tmpn = 512

    def dram(self, name, shape, kind):
        return self.nc.dram_tensor(name, list(shape), F32, kind=kind).ap()

    def wload(self, src_ap, shape):
        b = self.wb[self.wb_rr]
        self.wb_rr = (self.wb_rr + 1) % len(self.wb)
        n = 1
        for s in shape[1:]:
            n *= s
        if len(shape) == 3:
            view = b.t[:, 0:n].rearrange("p (a b) -> p a b", a=shape[1])
        else:
            view = b.t[:, 0:n]
        self.P.dma('sp', R(view), R(src_ap), w=[b.k])
        return view, b.k

    def tmpbuf(self, name, n=512, cnt=2):
        n = min(n, self.tmpn)
        if name not in self.tmp:
            self.tmp[name] = [[Buf(self.P, '%s%d' % (name, i), [128, n]) for i in range(cnt)], 0]
        lst = self.tmp[name]
        b = lst[0][lst[1]]
        lst[1] = (lst[1] + 1) % len(lst[0])
        return b


def pieces(n, m=512):
    out = []
    c = 0
    while c < n:
        out.append((c, min(m, n - c)))
        c += m
    return out


def rms_rstd(C, X, xks, c0, n, rstd, onesK, epsap, nk=KC):
    P = C.P
    ps, pk = P.next_ps()
    for k in range(nk):
        sq = C.tmpbuf('sq')
        P.op('act', 'activation', out=R(sq.t[:, 0:n]), in_=X[:, k, c0:c0 + n], func=AF.Square,
             r=[xks[k]], w=[sq.k])
        P.op('pe', 'matmul', ps[:, 0:n], lhsT=R(onesK[0]), rhs=R(sq.t[:, 0:n]),
                                                   start=(k == 0), stop=(k == nk - 1),
             r=[sq.k, onesK[1]], w=[pk])
    sd = C.tmpbuf('sd')
    P.op('act', 'activation', out=sd.t[:, 0:n], in_=ps[:, 0:n], func=AF.Sqrt, bias=epsap[0], scale=1.0,
         r=[pk, epsap[1]], w=[sd.k])
    P.op('dve', 'reciprocal', out=rstd.t[:, 0:n], in_=sd.t[:, 0:n], r=[sd.k], w=[rstd.k])


def norm_mod(C, X, xks, c0, n, H, hks, h0, A, SH, vk, onesK, epsap):
    P = C.P
    for (p0, pn) in pieces(n):
        rstd = C.tmpbuf('rstd')
        rms_rstd(C, X, xks, c0 + p0, pn, rstd, onesK, epsap)
        for k in range(KC):
            P.op('dve', 'tensor_tensor', out=R(H[:, k, h0 + p0:h0 + p0 + pn]), in0=X[:, k, c0 + p0:c0 + p0 + pn],
                                                       in1=rstd.t[:, 0:pn], op=ALU.mult,
                 r=[xks[k], rstd.k], w=[hks[k]])
            P.op('act', 'activation', out=R(H[:, k, h0 + p0:h0 + p0 + pn]), in_=H[:, k, h0 + p0:h0 + p0 + pn],
                                                    func=AF.Identity, scale=A[:, k:k + 1], bias=SH[:, k:k + 1],
                 r=[hks[k], vk], w=[hks[k]])


def linear_blk(C, wv, wk, nch, Hin, hks, h0, n, evac, nk=KC, f32r=True):
    P = C.P
    cv = R if f32r else (lambda a: a)
    for c in range(nch):
        ps, pk = P.next_ps()
        for k in range(nk):
            P.op('pe', 'matmul', ps[:, 0:n], lhsT=cv(wv[:, k, c * 128:(c + 1) * 128]),
                                                         rhs=cv(Hin[:, k, h0:h0 + n]), start=(k == 0), stop=(k == nk - 1),
                 r=[wk, hks[k]], w=[pk])
        evac(c, ps, pk)


def mlp(C, X, xks, c0, n, H, hks, HID, w1, w2, GT, vk):
    P = C.P
    w1v = w1.rearrange("(k p) c -> p k c", p=128)
    w2v = w2.rearrange("(j p) c -> p j c", p=128)
    for qf in range(4):
        for blk in range(2):
            col0 = qf * 1024 + blk * 512
            wv, wk = C.wload(w1v[:, :, col0:col0 + 512], [128, 8, 512])

            def ev(c, ps, pk, blk=blk):
                jj = blk * 4 + c
                P.op('act', 'activation', out=R(HID.t[:, jj, 0:n]), in_=ps[:, 0:n], func=AF.Relu,
                     r=[pk], w=[HID.ks[jj]])
                P.op('dve', 'tensor_tensor', out=R(HID.t[:, jj, 0:n]), in0=HID.t[:, jj, 0:n], in1=HID.t[:, jj, 0:n],
                                                       op=ALU.mult, r=[HID.ks[jj]], w=[HID.ks[jj]])
            linear_blk(C, wv, wk, 4, H, hks, 0, n, ev)
        for nh in range(2):
            wv, wk = C.wload(w2v[:, qf * 8:(qf + 1) * 8, nh * 512:(nh + 1) * 512], [128, 8, 512])

            def ev2(c, ps, pk, nh=nh):
                ch = nh * 4 + c
                P.op('dve', 'scalar_tensor_tensor', out=X[:, ch, c0:c0 + n], in0=ps[:, 0:n], scalar=GT[:, ch:ch + 1],
                                                             op0=ALU.mult, in1=X[:, ch, c0:c0 + n], op1=ALU.add,
                     r=[pk, vk, xks[ch]], w=[xks[ch]])
            linear_blk(C, wv, wk, 4, HID.t, HID.ks, 0, n, ev2)


def ada_vectors(C, ada_w, MOD, SC, vk_in, adab_ap):
    P = C.P
    nj = MOD.t.shape[1]
    wv3 = ada_w.rearrange("(k p) c -> p k c", p=128)
    for b in range(nj // 4):
        wv, wk = C.wload(wv3[:, :, b * 512:(b + 1) * 512], [128, 8, 512])

        def ev(c, ps, pk, b=b):
            j = b * 4 + c
            P.op('dve', 'tensor_scalar', out=MOD.t[:, j, 0:2], in0=ps[:, 0:2], scalar1=adab_ap[:, j:j + 1],
                                                  scalar2=None, op0=ALU.add, r=[pk, vk_in], w=[MOD.k])
        linear_blk(C, wv, wk, 4, SC.t, [SC.k] * 8, 0, 2, ev, f32r=False)


VA_C, VA_CC, VA_ADAB, VA_G1, VA_G2, VA_QG, VA_KG, VA_SINK, VA_N = 0, 8, 16, 64, 72, 80, 81, 82, 98


def build_A(NT):
    nc = bass.Bass("TRN2", target_bir_lowering=False)
    nc.dge_precook = False
    C = Ctx(nc)
    P = C.P
    TOK = NT * 512
    xT = C.dram("xT", [D, TOK + 256], "ExternalInput")
    cT = C.dram("cT", [D, 256], "ExternalInput")
    wqkv = C.dram("wqkv", [D, 1536], "ExternalInput")
    wo = C.dram("wo", [D, D], "ExternalInput")
    w1 = C.dram("w1", [D, 4096], "ExternalInput")
    w2 = C.dram("w2", [4096, D], "ExternalInput")
    adaw = C.dram("adaw", [D, 6144], "ExternalInput")
    vecs = C.dram("vecs", [128, VA_N], "ExternalInput")
    cst = C.dram("cst", [128, 8, 128], "ExternalInput")
    rope = C.dram("rope", [128, 2, TOK + 256], "ExternalInput")
    xo = C.dram("xo", [D, TOK], "ExternalOutput")
    co = C.dram("co", [D, 256], "ExternalOutput")

    X = Buf(P, 'X', [128, 8, 768], 8)
    H = Buf(P, 'H', [128, 8, 768], 8)
    Q = Buf(P, 'Q', [128, 8, 512], 8)
    KT = Buf(P, 'KT', [128, 2, 768], 2)
    VP = Buf(P, 'VP', [128, 6, 4, 128], 6)
    KTc = Buf(P, 'KTc', [128, 2, 256], 2)
    VPc = Buf(P, 'VPc', [128, 2, 4, 128], 2)
    HID = Buf(P, 'HID', [128, 8, 512], 8)
    VEC = Buf(P, 'VEC', [128, VA_N])
    CST = Buf(P, 'CST', [128, 8, 128])
    ROPE = Buf(P, 'ROPE', [128, 2, 768])
    SC = Buf(P, 'SC', [128, 8, 2])
    MOD = Buf(P, 'MOD', [128, 48, 2])
    DER = Buf(P, 'DER', [128, 4, 8])
    EPS = Buf(P, 'EPS', [128, 1])
    SK = Buf(P, 'SK', [128, 16])

    P.dma('sp', VEC.t[:], vecs, w=[VEC.k])
    P.dma('sp', R(CST.t[:]), R(cst), w=[CST.k])
    P.op('pool', 'memset', EPS.t[:], NORM_EPS, w=[EPS.k])
    ZT = Buf(P, 'ZT', [128, 512])
    P.op('pool', 'memset', ZT.t[:], 0.0, w=[ZT.k])
    for tb in range(6):
        P.op('dve', 'tensor_copy', out=R(VP.t[:, tb, :, :].rearrange("p a b -> p (a b)")), in_=ZT.t[:, :], r=[ZT.k], w=[VP.ks[tb]])
    for tb in range(2):
        P.op('dve', 'tensor_copy', out=R(VPc.t[:, tb, :, :].rearrange("p a b -> p (a b)")), in_=ZT.t[:, :], r=[ZT.k], w=[VPc.ks[tb]])
    P.op('act', 'activation', out=SC.t[:, :, 0], in_=VEC.t[:, VA_C:VA_C + 8], func=AF.Silu, r=[VEC.k], w=[SC.k])
    P.op('act', 'activation', out=SC.t[:, :, 1], in_=VEC.t[:, VA_CC:VA_CC + 8], func=AF.Silu, r=[VEC.k], w=[SC.k])
    P.op('act', 'activation', out=SK.t[:], in_=VEC.t[:, VA_SINK:VA_SINK + 16], func=AF.Exp, r=[VEC.k], w=[SK.k])
    ada_vectors(C, adaw, MOD, SC, VEC.k, VEC.t[:, VA_ADAB:VA_ADAB + 48])
    for i, (gcol, sccol) in enumerate(((VA_G1, 8), (VA_G2, 32))):
        for v in range(2):
            P.op('dve', 'scalar_tensor_tensor', out=DER.t[:, 2 * i + v, :], in0=MOD.t[:, sccol:sccol + 8, v], scalar=1.0, op0=ALU.add,
                in1=VEC.t[:, gcol:gcol + 8], op1=ALU.mult, r=[MOD.k, VEC.k], w=[DER.k])
    onesD = (CST.t[:, 0, :], CST.k)
    bones = (CST.t[:, 1, :], CST.k)
    RT = CST.t[:, 2, :]
    ones1 = CST.t[:, 3, :]
    epsap = (EPS.t[:, 0:1], EPS.k)
    xv = xT.rearrange("(k p) t -> p k t", p=128)
    cv_ = cT.rearrange("(k p) t -> p k t", p=128)
    xov = xo.rearrange("(k p) t -> p k t", p=128)
    cov = co.rearrange("(k p) t -> p k t", p=128)
    wqv = wqkv.rearrange("(k p) c -> p k c", p=128)
    wov = wo.rearrange("(k p) c -> p k c", p=128)

    def qknorm(buf, ks, ch, c0, n, gcol, rope_c0):
        qa = buf[:, ch, c0:c0 + n]
        sq = C.tmpbuf('sq')
        P.op('act', 'activation', out=R(sq.t[:, 0:n]), in_=qa, func=AF.Square, r=[ks[ch]], w=[sq.k])
        ps, pk = P.next_ps()
        P.op('pe', 'matmul', ps[:, 0:n], lhsT=R(bones[0]), rhs=R(sq.t[:, 0:n]), start=True, stop=True,
             r=[sq.k, CST.k], w=[pk])
        sd = C.tmpbuf('sd')
        P.op('act', 'activation', out=sd.t[:, 0:n], in_=ps[:, 0:n], func=AF.Sqrt, bias=epsap[0], scale=1.0,
             r=[pk, EPS.k], w=[sd.k])
        rs = C.tmpbuf('rs')
        P.op('dve', 'reciprocal', out=rs.t[:, 0:n], in_=sd.t[:, 0:n], r=[sd.k], w=[rs.k])
        P.op('dve', 'scalar_tensor_tensor', out=qa, in0=qa, scalar=VEC.t[:, gcol:gcol + 1], op0=ALU.mult,
                                                     in1=rs.t[:, 0:n], op1=ALU.mult, r=[ks[ch], VEC.k, rs.k], w=[ks[ch]])
        if rope_c0 is None:
            return
        ps2, pk2 = P.next_ps()
        P.op('pe', 'matmul', ps2[:, 0:n], lhsT=RT, rhs=qa, start=True, stop=True, r=[ks[ch], CST.k], w=[pk2])
        t1 = C.tmpbuf('t1')
        t2 = C.tmpbuf('t2')
        P.op('dve', 'tensor_tensor', out=t1.t[:, 0:n], in0=qa, in1=ROPE.t[:, 0, rope_c0:rope_c0 + n], op=ALU.mult,
             r=[ks[ch], ROPE.k], w=[t1.k])
        P.op('dve', 'tensor_tensor', out=t2.t[:, 0:n], in0=ps2[:, 0:n], in1=ROPE.t[:, 1, rope_c0:rope_c0 + n],
                                              op=ALU.mult, r=[pk2, ROPE.k], w=[t2.k])
        P.op('dve', 'tensor_tensor', out=qa, in0=t1.t[:, 0:n], in1=t2.t[:, 0:n], op=ALU.add,
             r=[t1.k, t2.k], w=[ks[ch]])

    def tile(src_v, s0, TT, cen0, ncen, v, KTb, VPb, is_ctx, first, last, dst_v, d0, rope_s0):
        A1 = DER.t[:, 0 + v, :]
        A2 = DER.t[:, 2 + v, :]
        SH1 = MOD.t[:, 0:8, v]
        GT1 = MOD.t[:, 16:24, v]
        SH2 = MOD.t[:, 24:32, v]
        GT2 = MOD.t[:, 40:48, v]
        vk = MOD.k
        P.dma('sp', X.t[:, :, 0:TT], src_v[:, :, s0:s0 + TT], w=X.ks)
        if not is_ctx:
            P.dma('sp', ROPE.t[:, :, 0:TT], rope[:, :, rope_s0:rope_s0 + TT], w=[ROPE.k])
        norm_mod(C, X.t, X.ks, 0, TT, H.t, H.ks, 0, A1, SH1, vk, onesD, epsap)
        for qb in range(2):
            wv, wk = C.wload(wqv[:, :, qb * 512:(qb + 1) * 512], [128, 8, 512])

            def evq(c, ps, pk, qb=qb):
                ch = qb * 4 + c
                P.op('act', 'activation', out=Q.t[:, ch, 0:ncen], in_=ps[:, 0:ncen], func=AF.Copy,
                     r=[pk], w=[Q.ks[ch]])
                qknorm(Q.t, Q.ks, ch, 0, ncen, VA_QG, None if is_ctx else cen0)
            linear_blk(C, wv, wk, 4, H.t, H.ks, cen0, ncen, evq)
        wv, wk = C.wload(wqv[:, :, 1024:1536], [128, 8, 512])
        for (p0, pn) in pieces(TT):
            def evk(c, ps, pk, p0=p0, pn=pn):
                P.op('act', 'activation', out=KTb.t[:, c, p0:p0 + pn], in_=ps[:, 0:pn], func=AF.Copy,
                     r=[pk], w=[KTb.ks[c]])
                qknorm(KTb.t, KTb.ks, c, p0, pn, VA_KG, None if is_ctx else p0)
            linear_blk(C, wv, wk, 2, H.t, H.ks, p0, pn, evk)
        for tb in range(TT // 128):
            ps, pk = P.next_ps()
            for k in range(KC):
                P.op('pe', 'matmul', ps[:, 0:256], lhsT=R(H.t[:, k, tb * 128:(tb + 1) * 128]),
                                                               rhs=R(wv[:, k, 256:512]), start=(k == 0), stop=(k == KC - 1),
                     r=[wk, H.ks[k]], w=[pk])
            psv = ps[:, 0:256].rearrange("p (g d) -> p g d", g=4)
            P.op('act', 'activation', out=R(VPb.t[:, tb, 0:4:2, 0:64]), in_=psv[:, 0:4:2, :], func=AF.Copy,
                 r=[pk], w=[VPb.ks[tb]])
            P.op('dve', 'tensor_copy', out=R(VPb.t[:, tb, 1:4:2, 64:128]), in_=psv[:, 1:4:2, :],
                 r=[pk], w=[VPb.ks[tb]])
        nqb = ncen // 128
        for qb in range(nqb):
            for g in range(4):
                po = 64 * (g % 2)
                sl = 4 * (g // 2)
                qrhs = Q.t[po:po + 64, sl:sl + 4, qb * 128:(qb + 1) * 128]
                qks = [Q.ks[sl + j] for j in range(4)]
                kbs = []
                if not is_ctx:
                    for w_ in range(3):
                        tb = qb + w_
                        m = None
                        if w_ == 0:
                            m = 6 if (first and qb == 0) else 4
                        if w_ == 2:
                            m = 7 if (last and qb == nqb - 1) else 5
                        kbs.append((KT, VP, tb, m))
                for tb in range(2):
                    kbs.append((KTc, VPc, tb, None))
                pts = []
                for (kb, vb, tb, m) in kbs:
                    ps, pk = P.next_ps()
                    P.op('pe', 'matmul', ps[:, :], lhsT=kb.t[po:po + 64, g // 2, tb * 128:(tb + 1) * 128],
                                                                     rhs=qrhs, start=True, stop=True,
                         r=[kb.ks[g // 2]] + qks, w=[pk])
                    pt = C.tmpbuf('pt', 512, 6)
                    P.op('act', 'activation', out=R(pt.t[:, :]), in_=ps[:, :], func=AF.Exp, scale=0.125,
                         r=[pk], w=[pt.k])
                    if m is not None:
                        mk = CST.t[:, m, :].unsqueeze(1).to_broadcast([128, 4, 128])
                        P.op('dve', 'tensor_tensor', out=R(pt.t[:, :].rearrange("p (a b) -> p a b", a=4)), in0=pt.t[:, :].rearrange("p (a b) -> p a b", a=4),
                            in1=mk, op=ALU.mult, r=[pt.k, CST.k], w=[pt.k])
                    pts.append((pt, vb, tb))
                pso, pko = P.next_ps()
                psd, pkd = P.next_ps()
                for i, (pt, vb, tb) in enumerate(pts):
                    P.op('pe', 'matmul', pso[:, :], lhsT=R(vb.t[:, tb, g, :]), rhs=R(pt.t[:, :]),
                                                                          start=(i == 0), stop=(i == len(pts) - 1),
                         r=[pt.k, vb.ks[tb]], w=[pko])
                    P.op('pe', 'matmul', psd[:, :], lhsT=R(ones1), rhs=R(pt.t[:, :]),
                                                            start=(i == 0), stop=(i == len(pts) - 1),
                         r=[pt.k, CST.k], w=[pkd])
                dn = C.tmpbuf('dn')
                skb = SK.t[po:po + 64, 4 * g:4 * g + 4].unsqueeze(2).to_broadcast([64, 4, 128])
                dnv = dn.t[po:po + 64, :].rearrange("p (a b) -> p a b", a=4)
                P.op('dve', 'tensor_tensor', out=dnv, in0=psd[po:po + 64, :].rearrange("p (a b) -> p a b", a=4),
                                                      in1=skb, op=ALU.add, r=[pkd, SK.k], w=[dn.k])
                P.op('dve', 'reciprocal', out=dn.t[po:po + 64, :], in_=dn.t[po:po + 64, :], r=[dn.k], w=[dn.k])
                hks_o = [H.ks[sl + j] for j in range(4)]
                P.op('dve', 'tensor_tensor', out=R(H.t[po:po + 64, sl:sl + 4, qb * 128:(qb + 1) * 128]),
                                                      in0=pso[po:po + 64, :].rearrange("p (a b) -> p a b", a=4),
                                                      in1=dnv, op=ALU.mult, r=[pko, dn.k], w=hks_o)
        for nb in range(2):
            wv, wk = C.wload(wov[:, :, nb * 512:(nb + 1) * 512], [128, 8, 512])

            def evo(c, ps, pk, nb=nb):
                ch = nb * 4 + c
                P.op('dve', 'scalar_tensor_tensor', out=X.t[:, ch, cen0:cen0 + ncen], in0=ps[:, 0:ncen],
                                                             scalar=GT1[:, ch:ch + 1], op0=ALU.mult,
                                                             in1=X.t[:, ch, cen0:cen0 + ncen], op1=ALU.add,
                     r=[pk, vk, X.ks[ch]], w=[X.ks[ch]])
            linear_blk(C, wv, wk, 4, H.t, H.ks, 0, ncen, evo)
        norm_mod(C, X.t, X.ks, cen0, ncen, H.t, H.ks, 0, A2, SH2, vk, onesD, epsap)
        mlp(C, X.t, X.ks, cen0, ncen, H.t, H.ks, HID, w1, w2, GT2, vk)
        ok = Trk()
        P.dma('sp', dst_v[:, :, d0:d0 + ncen], X.t[:, :, cen0:cen0 + ncen], r=X.ks, w=[ok])
        return ok

    outs = [tile(cv_, 0, 256, 0, 256, 1, KTc, VPc, True, False, False, cov, 0, None)]
    for i in range(NT):
        outs.append(tile(xv, i * 512, 768, 128, 512, 0, KT, VP, False, i == 0, i == NT - 1, xov, i * 512, i * 512))
    P.wait_all('sp', outs)
    P.finish()
    return nc


def rope_tables(pos_rows, pos_cols):
    n = len(pos_rows)
    f = 10000.0 ** (-np.arange(16, dtype=np.float32) / 16.0)
    ang = np.zeros((64, n), np.float32)
    ar = pos_rows.astype(np.float32)[None, :] * f[:, None]
    ac = pos_cols.astype(np.float32)[None, :] * f[:, None]
    ang[0:16] = ar
    ang[16:32] = ar
    ang[32:48] = ac
    ang[48:64] = ac
    cos = np.cos(ang).astype(np.float32)
    sin = np.sin(ang).astype(np.float32)
    return np.concatenate([cos, cos], 0), np.concatenate([sin, sin], 0)


def consts_A():
    cst = np.zeros((128, 8, 128), np.float32)
    cst[:, 0, :] = 1.0 / D
    for h in range(2):
        cst[64 * h:64 * h + 64, 1, 64 * h:64 * h + 64] = 1.0 / 64
    for h in range(2):
        b = 64 * h
        for i in range(64):
            blk = i // 16
            if blk % 2 == 0:
                cst[b + i + 16, 2, b + i] = -1.0
            else:
                cst[b + i - 16, 2, b + i] = 1.0
    cst[:, 3, :] = 1.0
    j = np.arange(128)[:, None]
    i = np.arange(128)[None, :]
    cst[:, 4, :] = (j >= i)
    cst[:, 5, :] = (j <= i)
    return cst


def q_perm():
    idx = []
    for gp in range(2):
        for j in range(4):
            for half in range(2):
                h = 4 * (2 * gp + half) + j
                idx.extend(range(64 * h, 64 * h + 64))
    return np.array(idx)


def colvec(v):
    return np.ascontiguousarray(v.reshape(-1, 128).T)


def run_A(x, c, ctx, c_ctx, ada_w0, ada_b0, n1g, n2g, w_qkv, qg, kg, sink, w_o, w1, w2, cores_per_batch, grid_w=64):
    B, S, _ = x.shape
    cpb = cores_per_batch
    TOK = S // cpb
    NT = TOK // 512
    nc = build_A(NT)
    perm = q_perm()
    wqkv_p = np.ascontiguousarray(np.concatenate([w_qkv[:, perm], w_qkv[:, 1024:]], 1))
    wo_p = np.ascontiguousarray(w_o[perm, :])
    cst0 = consts_A()
    pos = np.arange(S)
    rows, cols = pos // grid_w, pos % grid_w
    cosT, sinT = rope_tables(rows, cols)
    in_maps = []
    for b in range(B):
        for hf in range(cpb):
            t0 = hf * TOK
            xpad = np.zeros((TOK + 256, D), np.float32)
            lo, hi = max(t0 - 128, 0), min(t0 + TOK + 128, S)
            xpad[lo - (t0 - 128):hi - (t0 - 128)] = x[b, lo:hi]
            rp = np.zeros((128, 2, TOK + 256), np.float32)
            rp[:, 0, lo - (t0 - 128):hi - (t0 - 128)] = cosT[:, lo:hi]
            rp[:, 1, lo - (t0 - 128):hi - (t0 - 128)] = sinT[:, lo:hi]
            cst = cst0.copy()
            cst[:, 6, :] = cst0[:, 4, :] if t0 > 0 else 0.0
            cst[:, 7, :] = cst0[:, 5, :] if t0 + TOK < S else 0.0
            vecs = np.zeros((128, VA_N), np.float32)
            vecs[:, VA_C:VA_C + 8] = colvec(c[b])
            vecs[:, VA_CC:VA_CC + 8] = colvec(c_ctx)
            vecs[:, VA_ADAB:VA_ADAB + 48] = colvec(ada_b0)
            vecs[:, VA_G1:VA_G1 + 8] = colvec(n1g)
            vecs[:, VA_G2:VA_G2 + 8] = colvec(n2g)
            vecs[:, VA_QG] = np.concatenate([qg, qg])
            vecs[:, VA_KG] = np.concatenate([kg, kg])
            vecs[:, VA_SINK:VA_SINK + 16] = sink[None, :]
            in_maps.append({"xT": np.ascontiguousarray(xpad.T), "cT": np.ascontiguousarray(ctx[b].T), "wqkv": wqkv_p, "wo": wo_p,
                            "w1": w1, "w2": w2, "adaw": ada_w0, "vecs": vecs, "cst": cst, "rope": rp})
    return nc, in_maps


VB_C, VB_CC, VB_ADAB, VB_G1, VB_MU, VB_OWN, VB_N = 0, 8, 16, 32, 40, 88, 120
SEGB = 256
DEC_K = -0.6065306597126334


def build_B(NSEG):
    nc = bass.Bass("TRN2", target_bir_lowering=False)
    nc.dge_precook = False
    C = Ctx(nc)
    C.tmpn = SEGB + 8
    P = C.P
    n = SEGB
    NCH = n // 64
    TOK = NSEG * n
    xT = C.dram("xT", [D, TOK + 2], "ExternalInput")
    cT = C.dram("cT", [D, n + 2], "ExternalInput")
    adaw = C.dram("adaw", [D, 2048], "ExternalInput")
    wr = C.dram("wr", [D, 512], "ExternalInput")
    wk_ = C.dram("wk", [D, 512], "ExternalInput")
    wv_ = C.dram("wv", [D, 512], "ExternalInput")
    lw1 = C.dram("lw1", [D, 448], "ExternalInput")
    w2o = C.dram("w2o", [128, 512], "ExternalInput")
    a2o = C.dram("a2o", [64, 512], "ExternalInput")
    g2o = C.dram("g2o", [128, 2, 512], "ExternalInput")
    vecs = C.dram("vecs", [128, VB_N], "ExternalInput")
    cst = C.dram("cst", [128, 5, 128], "ExternalInput")
    msk = C.dram("msk", [64, 4, 64], "ExternalInput")
    smk = C.dram("smk", [128, 4 * n], "ExternalInput")
    of_ = C.dram("of", [512, TOK], "ExternalOutput")
    ob_ = C.dram("ob", [512, TOK], "ExternalOutput")

    X = Buf(P, 'X', [128, 8, n + 2], 1)
    H = Buf(P, 'H', [128, 8, n + 2], 1)
    XX = Buf(P, 'XX', [128, 8, n], 1)
    names = ['RF', 'KF', 'VF', 'LW', 'IC', 'G', 'A', 'B', 'CS', 'D1', 'D2', 'XN', 'BB', 'KB']
    F = {nm: Buf(P, nm, [128, 4, n], 4) for nm in names}
    TOT = Buf(P, 'TOT', [128, 4, NCH])
    WC = Buf(P, 'WC', [128, 4, NCH])
    ST = Buf(P, 'ST', [128, 4, 64])
    VEC = Buf(P, 'VEC', [128, VB_N])
    CST = Buf(P, 'CST', [128, 5, 128])
    MSK = Buf(P, 'MSK', [64, 4, 64])
    SMK = Buf(P, 'SMK', [128, 4 * n])
    SC = Buf(P, 'SC', [128, 8, 2])
    MOD = Buf(P, 'MOD', [128, 16, 2])
    DER = Buf(P, 'DER', [128, 2, 8])
    EPS = Buf(P, 'EPS', [128, 2])
    W2O = Buf(P, 'W2O', [128, 512])
    A2O = Buf(P, 'A2O', [64, 512])
    G2O = Buf(P, 'G2O', [128, 2, 512])
    T1 = Buf(P, 'T1b', [128, n])
    TA = Buf(P, 'TAb', [64, n])
    TG = Buf(P, 'TGb', [128, n])
    tokm = {nm: [Buf(P, nm + str(i), [64, 512]) for i in range(1)] for nm in ('BT', 'KTk', 'VT')}
    mats = {nm: Buf(P, nm, [64, 512]) for nm in ('AakT', 'ArkT', 'ArbT', 'X1', 'U')}
    gens = {nm: [Buf(P, nm + str(i), [64, 512]) for i in range(2)] for nm in ('Pm', 'PTm', 'TT')}

    P.dma('sp', VEC.t[:], vecs, w=[VEC.k])
    P.dma('sp', R(CST.t[:]), R(cst), w=[CST.k])
    P.dma('sp', MSK.t[:], msk, w=[MSK.k])
    P.dma('sp', SMK.t[:], smk, w=[SMK.k])
    lw1v = lw1.rearrange("(k p) c -> p k c", p=128)
    P.dma('sp', W2O.t[:], w2o, w=[W2O.k])
    P.dma('sp', A2O.t[:], a2o, w=[A2O.k])
    P.dma('sp', G2O.t[:], g2o, w=[G2O.k])
    P.op('pool', 'memset', EPS.t[:, 0:1], NORM_EPS, w=[EPS.k])
    P.op('pool', 'memset', EPS.t[:, 1:2], GN_EPS, w=[EPS.k])
    P.op('act', 'activation', out=SC.t[:, :, 0], in_=VEC.t[:, VB_C:VB_C + 8], func=AF.Silu, r=[VEC.k], w=[SC.k])
    P.op('act', 'activation', out=SC.t[:, :, 1], in_=VEC.t[:, VB_CC:VB_CC + 8], func=AF.Silu, r=[VEC.k], w=[SC.k])
    ada_vectors(C, adaw, MOD, SC, VEC.k, VEC.t[:, VB_ADAB:VB_ADAB + 16])
    for v in range(2):
        P.op('dve', 'scalar_tensor_tensor', out=DER.t[:, v, :], in0=MOD.t[:, 8:16, v], scalar=1.0, op0=ALU.add,
             in1=VEC.t[:, VB_G1:VB_G1 + 8], op1=ALU.mult, r=[MOD.k, VEC.k], w=[DER.k])
    onesD = (CST.t[:, 0, :], CST.k)
    bones64 = CST.t[:, 1, :]
    bones1 = CST.t[:, 2, :]
    ident = CST.t[:, 3, :]
    epsap = (EPS.t[:, 0:1], EPS.k)
    xv = xT.rearrange("(k p) t -> p k t", p=128)
    cv_ = cT.rearrange("(k p) t -> p k t", p=128)
    wrv = wr.rearrange("(k p) c -> p k c", p=128)
    wkv = wk_.rearrange("(k p) c -> p k c", p=128)
    wvv = wv_.rearrange("(k p) c -> p k c", p=128)
    ofv = of_.rearrange("(k p) t -> p k t", p=128)
    obv = ob_.rearrange("(k p) t -> p k t", p=128)

    def own(slot, c):
        col = VB_OWN + 4 * slot + c
        return VEC.t[:, col:col + 1]

    def b8(ap64):
        return ap64.unsqueeze(1).to_broadcast([64, 8, 64])

    def v8(t):
        return t[:, :].rearrange("p (a b) -> p a b", a=8)

    def segment(d, src_v, s0, v, is_ctx, dst_v, d0):
        RF, KF, VF, LW, IC, G, A, B, CS, D1, D2, XN, BB, KB = [F[nm] for nm in names]
        allx = X.ks
        P.dma('sp', R(X.t[:, :, :]), R(src_v[:, :, s0:s0 + n + 2]), w=allx)
        Xc = [X.k] * 8
        Hc = [H.k] * 8
        norm_mod(C, X.t, Xc, 0, n + 2, H.t, Hc, 0, DER.t[:, v, :], MOD.t[:, 0:8, v], MOD.k, onesD, epsap)
        first = is_ctx or s0 == 0
        last = is_ctx or s0 + n == TOK
        if first:
            P.op('pool', 'memset', H.t[:, :, 0:1], 0.0, w=[H.k])
        if last:
            P.op('pool', 'memset', H.t[:, :, n + 1:n + 2], 0.0, w=[H.k])
        P.op('pool', 'tensor_tensor', out=XX.t[:, :, :], in0=H.t[:, :, 0:n], in1=H.t[:, :, 2:n + 2], op=ALU.add, r=[H.k], w=[XX.k])
        P.op('dve', 'scalar_tensor_tensor', out=XX.t[:, :, :], in0=XX.t[:, :, :], scalar=0.5, op0=ALU.mult,
             in1=H.t[:, :, 1:n + 1], op1=ALU.subtract, r=[H.k, XX.k], w=[XX.k])

        def mix(m):
            mub = VEC.t[:, VB_MU + 8 * m:VB_MU + 8 * m + 8].unsqueeze(2).to_broadcast([128, 8, n])
            P.op('dve', 'tensor_tensor', out=R(X.t[:, :, 0:n]), in0=XX.t[:, :, :], in1=mub, op=ALU.mult, r=[XX.k, VEC.k], w=[X.k])
            P.op('dve', 'tensor_tensor', out=R(X.t[:, :, 0:n]), in0=X.t[:, :, 0:n], in1=H.t[:, :, 1:n + 1], op=ALU.add,
                 r=[X.k, H.k], w=[X.k])

        def proj(wview, dst):
            wv, wk = C.wload(wview, [128, 8, 512])

            def ev(c, ps, pk):
                P.op('act', 'activation', out=dst.t[:, c, :], in_=ps[:, 0:n], func=AF.Copy, r=[pk], w=[dst.ks[c]])
            linear_blk(C, wv, wk, 4, X.t, Xc, 0, n, ev)

        def lora(w1t, w1k, c0, m1, po, tbuf, func1, w2t, w2k, w2cols, evac2):
            ps, pk = P.next_ps()
            for k in range(KC):
                P.op('pe', 'matmul', ps[po:po + m1, 0:n], lhsT=w1t[:, k, c0:c0 + m1], rhs=X.t[:, k, 0:n], start=(k == 0), stop=(k == KC - 1),
                     r=[w1k, X.k], w=[pk])
            P.op('act', 'activation', out=tbuf.t[po:po + m1, :], in_=ps[po:po + m1, 0:n], func=func1, r=[pk], w=[tbuf.k])
            for c in range(4):
                ps2, pk2 = P.next_ps()
                P.op('pe', 'matmul', ps2[:, 0:n], lhsT=w2cols(c), rhs=tbuf.t[po:po + m1, :], start=True, stop=True,
                     r=[w2k, tbuf.k], w=[pk2])
                evac2(c, ps2, pk2)

        mix(0)
        proj(wrv, RF)
        mix(2)
        proj(wkv, KF)
        mix(3)
        proj(wvv, VF)
        LWt, LWk = C.wload(lw1v, [128, 8, 448])
        mix(1)

        def ev_lw(c, ps, pk):
            P.op('act', 'activation', out=LW.t[:, c, :], in_=ps[:, 0:n], func=AF.Sigmoid, bias=own(d, c), scale=1.0,
                 r=[pk, VEC.k], w=[LW.ks[c]])
        lora(LWt, LWk, 64 * d, 64, 64 * d, T1, AF.Tanh, W2O.t, W2O.k,
             lambda c: W2O.t[64 * d:64 * d + 64, c * 128:(c + 1) * 128], ev_lw)
        P.op('dve', 'tensor_scalar', out=LW.t[:, :, :], in0=LW.t[:, :, :], scalar1=DEC_K, scalar2=None, op0=ALU.mult, r=LW.ks, w=LW.ks)
        mix(4)

        def ev_ic(c, ps, pk):
            P.op('act', 'activation', out=IC.t[:, c, :], in_=ps[:, 0:n], func=AF.Sigmoid, bias=own(2, c), scale=1.0,
                 r=[pk, VEC.k], w=[IC.ks[c]])
        lora(LWt, LWk, 128, 64, 0, TA, AF.Copy, A2O.t, A2O.k, lambda c: A2O.t[0:64, c * 128:(c + 1) * 128], ev_ic)
        if not is_ctx:
            mix(5)

            def ev_g(c, ps, pk):
                P.op('act', 'activation', out=G.t[:, c, :], in_=ps[:, 0:n], func=AF.Copy, r=[pk], w=[G.ks[c]])
            lora(LWt, LWk, 192 + 128 * d, 128, 0, TG, AF.Sigmoid, G2O.t, G2O.k,
                 lambda c: G2O.t[:, d, c * 128:(c + 1) * 128], ev_g)
        for c in range(4):
            kkb = C.tmpbuf('t1', 512, 2)
            P.op('dve', 'tensor_scalar', out=kkb.t[:, 0:n], in0=KF.t[:, c, :], scalar1=own(3, c), scalar2=None, op0=ALU.mult,
                 r=[KF.ks[c], VEC.k], w=[kkb.k])
            sq = C.tmpbuf('sq')
            P.op('act', 'activation', out=R(sq.t[:, 0:n]), in_=kkb.t[:, 0:n], func=AF.Square, r=[kkb.k], w=[sq.k])
            ps, pk = P.next_ps()
            P.op('pe', 'matmul', ps[:, 0:n], lhsT=bones1, rhs=sq.t[:, 0:n], start=True, stop=True, r=[sq.k, CST.k], w=[pk])
            sd = C.tmpbuf('sd', 512, 1)
            P.op('act', 'activation', out=sd.t[:, 0:n], in_=ps[:, 0:n], func=AF.Sqrt, r=[pk], w=[sd.k])
            P.op('dve', 'tensor_scalar', out=sd.t[:, 0:n], in0=sd.t[:, 0:n], scalar1=1e-12, scalar2=None, op0=ALU.max, r=[sd.k], w=[sd.k])
            rs = C.tmpbuf('rs', 512, 1)
            P.op('dve', 'reciprocal', out=rs.t[:, 0:n], in_=sd.t[:, 0:n], r=[sd.k], w=[rs.k])
            P.op('dve', 'scalar_tensor_tensor', out=A.t[:, c, :], in0=kkb.t[:, 0:n], scalar=-1.0, op0=ALU.mult, in1=rs.t[:, 0:n],
                 op1=ALU.mult, r=[kkb.k, rs.k], w=[A.ks[c]])
            P.op('dve', 'scalar_tensor_tensor', out=B.t[:, c, :], in0=A.t[:, c, :], scalar=-1.0, op0=ALU.mult, in1=IC.t[:, c, :],
                 op1=ALU.mult, r=[A.ks[c], IC.ks[c]], w=[B.ks[c]])
            km = C.tmpbuf('t2', 512, 2)
            P.op('dve', 'tensor_scalar', out=km.t[:, 0:n], in0=IC.t[:, c, :], scalar1=-1.0, scalar2=own(4, c), op0=ALU.add, op1=ALU.mult,
                 r=[IC.ks[c], VEC.k], w=[km.k])
            P.op('dve', 'scalar_tensor_tensor', out=KF.t[:, c, :], in0=km.t[:, 0:n], scalar=1.0, op0=ALU.add, in1=KF.t[:, c, :],
                 op1=ALU.mult, r=[km.k, KF.ks[c]], w=[KF.ks[c]])
            if not is_ctx:
                P.op('dve', 'scalar_tensor_tensor', out=km.t[:, 0:n], in0=RF.t[:, c, :], scalar=own(5, c), op0=ALU.mult, in1=KF.t[:, c, :],
                     op1=ALU.mult, r=[RF.ks[c], KF.ks[c], VEC.k], w=[km.k])
                ps, pk = P.next_ps()
                P.op('pe', 'matmul', ps[:, 0:n], lhsT=bones1, rhs=km.t[:, 0:n], start=True, stop=True, r=[km.k, CST.k], w=[pk])
                P.op('dve', 'tensor_tensor', out=IC.t[:, c, :], in0=ps[:, 0:n], in1=VF.t[:, c, :], op=ALU.mult,
                     r=[pk, VF.ks[c], B.ks[c]], w=[IC.ks[c]])
        fl = lambda b_: b_.t[:, :, :].rearrange("p a b -> p (a b)")
        P.op('dve', 'tensor_tensor_scan', out=fl(CS), data0=SMK.t[:, :], data1=fl(LW), initial=0.0, op0=ALU.mult, op1=ALU.add,
             r=LW.ks + [SMK.k], w=CS.ks)
        csv = CS.t[:, :, :].rearrange("p a (c t) -> p a c t", t=64)
        P.op('act', 'activation', out=TOT.t[:, :, :], in_=csv[:, :, :, 63], func=AF.Copy, r=CS.ks, w=[TOT.k])
        P.op('act', 'activation', out=WC.t[:, :, :], in_=TOT.t[:, :, :], func=AF.Exp, r=[TOT.k], w=[WC.k])
        P.op('pool', 'tensor_tensor', out=fl(D1), in0=fl(CS), in1=fl(LW), op=ALU.subtract, r=CS.ks + LW.ks, w=D1.ks)
        totb = TOT.t[:, :, :].unsqueeze(3).to_broadcast([128, 4, NCH, 64])
        P.op('dve', 'tensor_tensor', out=D2.t[:, :, :].rearrange("p a (c t) -> p a c t", t=64), in0=totb, in1=csv, op=ALU.subtract,
             r=CS.ks + [TOT.k], w=D2.ks)
        if d == 0:
            E1, E2, E3 = CS, D1, D2
        else:
            P.op('pool', 'tensor_tensor', out=fl(LW), in0=fl(D2), in1=fl(LW), op=ALU.add, r=D2.ks + LW.ks, w=LW.ks)
            E1, E2, E3 = LW, D2, D1
        P.op('act', 'activation', out=fl(XN), in_=fl(E1), func=AF.Exp, scale=-1.0, r=E1.ks, w=XN.ks)
        P.op('act', 'activation', out=fl(E1), in_=fl(E1), func=AF.Exp, r=E1.ks, w=E1.ks)
        P.op('act', 'activation', out=fl(E2), in_=fl(E2), func=AF.Exp, r=E2.ks, w=E2.ks)
        P.op('act', 'activation', out=fl(E3), in_=fl(E3), func=AF.Exp, r=E3.ks, w=E3.ks)
        P.op('dve', 'tensor_tensor', out=fl(RF), in0=fl(RF), in1=fl(E1), op=ALU.mult, r=RF.ks + E1.ks + IC.ks, w=RF.ks)
        P.op('pool', 'tensor_tensor', out=fl(A), in0=fl(A), in1=fl(E2), op=ALU.mult, r=A.ks + E2.ks + B.ks, w=A.ks)
        P.op('dve', 'tensor_tensor', out=fl(BB), in0=fl(B), in1=fl(E3), op=ALU.mult, r=B.ks + E3.ks, w=BB.ks)
        P.op('pool', 'tensor_tensor', out=fl(KB), in0=fl(KF), in1=fl(E3), op=ALU.mult, r=KF.ks + E3.ks + IC.ks, w=KB.ks)
        P.op('dve', 'tensor_tensor', out=fl(B), in0=fl(B), in1=fl(XN), op=ALU.mult, r=B.ks + XN.ks + BB.ks, w=B.ks)
        P.op('pool', 'tensor_tensor', out=fl(KF), in0=fl(KF), in1=fl(XN), op=ALU.mult, r=KF.ks + XN.ks + KB.ks, w=KF.ks)
        YF = XN
        if d == 0:
            mS, mI, mL = MSK.t[:, 0, :], MSK.t[:, 1, :], MSK.t[:, 2, :]
        else:
            mS, mI, mL = MSK.t[:, 2, :], MSK.t[:, 3, :], MSK.t[:, 0, :]
        corder = list(range(NCH)) if d == 0 else list(range(NCH - 1, -1, -1))
        for ci in corder:
            c0 = ci * 64
            cs_ = slice(c0, c0 + 64)
            tm = {}
            for nm, src in (('BT', BB), ('KTk', KB), ('VT', VF)):
                dst = tokm[nm][0]
                ps, pk = P.next_ps()
                for hc in range(4):
                    P.op('pe', 'transpose', out=ps[0:64, hc * 128:(hc + 1) * 128], in_=src.t[:, hc, cs_], identity=ident,
                         r=[src.ks[hc], CST.k], w=[pk])
                P.op('act', 'activation', out=dst.t[:, :].rearrange("p (q c k) -> p q c k", q=2, c=4),
                     in_=ps[0:64, :].rearrange("p (c q k) -> p q c k", c=4, q=2), func=AF.Copy, r=[pk], w=[dst.k])
                tm[nm] = dst
            BT, KTk, VT = tm['BT'], tm['KTk'], tm['VT']

            def hp(h):
                cb = (h % 2) * 4 + h // 2
                return h // 2, 64 * (h % 2), slice(cb * 64, (cb + 1) * 64)

            def b4(ap64):
                return ap64.unsqueeze(1).to_broadcast([64, 4, 64])

            def prod(lhs, rhs, mask, dst):
                for par in range(2):
                    ps, pk = P.next_ps()
                    for hc in range(4):
                        po = 64 * par
                        P.op('pe', 'matmul', ps[0:64, hc * 64:(hc + 1) * 64], lhsT=lhs.t[po:po + 64, hc, cs_],
                             rhs=rhs.t[po:po + 64, hc, cs_], start=True, stop=True, r=[lhs.ks[hc], rhs.ks[hc]], w=[pk])
                    P.op('dve', 'tensor_tensor', out=dst.t[:, par * 256:(par + 1) * 256].rearrange("p (a b) -> p a b", a=4),
                         in0=ps[0:64, 0:256].rearrange("p (a b) -> p a b", a=4), in1=b4(mask), op=ALU.mult,
                         r=[pk, MSK.k], w=[dst.k])
            AakT, ArkT, ArbT, X1, U = [mats[nm] for nm in ('AakT', 'ArkT', 'ArbT', 'X1', 'U')]
            Pm, PTm, TT = gens['Pm'], gens['PTm'], gens['TT']
            prod(KF, A, mS, AakT)
            prod(B, A, mS, Pm[0])
            prod(A, B, mL, PTm[0])
            if not is_ctx:
                prod(KF, RF, mI, ArkT)
                prod(B, RF, mI, ArbT)
            P.op('dve', 'tensor_tensor', out=v8(TT[0].t), in0=v8(Pm[0].t), in1=b8(ident[0:64, 0:64]), op=ALU.add,
                 r=[Pm[0].k, CST.k], w=[TT[0].k])
            gcur = 0
            for lev in range(1, 6):
                gn_ = 1 - gcur
                if lev < 5:
                    ps, pk = P.next_ps()
                    for h in range(8):
                        hs = slice(h * 64, (h + 1) * 64)
                        P.op('pe', 'matmul', ps[0:64, hs], lhsT=PTm[gcur].t[:, hs], rhs=Pm[gcur].t[:, hs], start=True, stop=True,
                             r=[PTm[gcur].k, Pm[gcur].k], w=[pk])
                    P.op('act', 'activation', out=Pm[gn_].t[:, :], in_=ps[0:64, :], func=AF.Copy, r=[pk], w=[Pm[gn_].k])
                ps, pk = P.next_ps()
                for h in range(8):
                    hs = slice(h * 64, (h + 1) * 64)
                    P.op('pe', 'matmul', ps[0:64, hs], lhsT=Pm[gcur].t[:, hs], rhs=PTm[gcur].t[:, hs], start=True, stop=True,
                         r=[PTm[gcur].k, Pm[gcur].k], w=[pk])
                P.op('act', 'activation', out=PTm[gn_].t[:, :], in_=ps[0:64, :], func=AF.Copy, r=[pk], w=[PTm[gn_].k])
                ps, pk = P.next_ps()
                for h in range(8):
                    hs = slice(h * 64, (h + 1) * 64)
                    P.op('pe', 'matmul', ps[0:64, hs], lhsT=PTm[gn_].t[:, hs], rhs=TT[gcur].t[:, hs], start=True, stop=True,
                         r=[PTm[gn_].k, TT[gcur].k], w=[pk])
                P.op('dve', 'tensor_tensor', out=TT[gn_].t[:, :], in0=ps[0:64, :], in1=TT[gcur].t[:, :], op=ALU.add,
                     r=[pk, TT[gcur].k], w=[TT[gn_].k])
                gcur = gn_
            TTf = TT[gcur]
            ps, pk = P.next_ps()
            psO, pkO = P.next_ps()
            for h in range(8):
                hc, po, hs = hp(h)
                if po == 0:
                    P.op('pe', 'matmul', ps[0:64, hs], lhsT=A.t[0:64, hc, cs_], rhs=ST.t[0:64, hc, :], start=True, stop=False,
                         r=[A.ks[hc], ST.k], w=[pk])
                else:
                    P.op('pe', 'matmul', psO[0:64, hc * 64:(hc + 1) * 64], lhsT=A.t[64:128, hc, cs_], rhs=ST.t[64:128, hc, :],
                         start=True, stop=True, r=[A.ks[hc], ST.k], w=[pkO])
                P.op('pe', 'matmul', ps[0:64, hs], lhsT=AakT.t[:, hs], rhs=VT.t[:, hs], start=(po != 0), stop=True,
                     r=[AakT.k, VT.k], w=[pk])
            P.op('act', 'activation', out=X1.t[:, :], in_=ps[0:64, :], func=AF.Copy, r=[pk], w=[X1.k])
            P.op('dve', 'tensor_tensor', out=X1.t[:, 256:512], in0=psO[0:64, 0:256], in1=X1.t[:, 256:512], op=ALU.add,
                 r=[pkO, X1.k], w=[X1.k])
            ps, pk = P.next_ps()
            for h in range(8):
                hs = slice(h * 64, (h + 1) * 64)
                P.op('pe', 'matmul', ps[0:64, hs], lhsT=TTf.t[:, hs], rhs=X1.t[:, hs], start=True, stop=True, r=[TTf.k, X1.k], w=[pk])
            P.op('act', 'activation', out=U.t[:, :], in_=ps[0:64, :], func=AF.Copy, r=[pk], w=[U.k])
            if not is_ctx:
                ps, pk = P.next_ps()
                psO, pkO = P.next_ps()
                for h in range(8):
                    hc, po, hs = hp(h)
                    o_ = ps[po:po + 64, hc * 64:(hc + 1) * 64]
                    if po == 0:
                        P.op('pe', 'matmul', o_, lhsT=ST.t[0:64, hc, :], rhs=RF.t[0:64, hc, cs_], start=True, stop=False,
                             r=[ST.k, RF.ks[hc]], w=[pk])
                    else:
                        P.op('pe', 'matmul', psO[64:128, hc * 64:(hc + 1) * 64], lhsT=ST.t[64:128, hc, :], rhs=RF.t[64:128, hc, cs_],
                             start=True, stop=True, r=[ST.k, RF.ks[hc]], w=[pkO])
                    P.op('pe', 'matmul', o_, lhsT=U.t[:, hs], rhs=ArbT.t[:, hs], start=(po != 0), stop=False, r=[U.k, ArbT.k], w=[pk])
                    P.op('pe', 'matmul', o_, lhsT=VT.t[:, hs], rhs=ArkT.t[:, hs], start=False, stop=True, r=[VT.k, ArkT.k], w=[pk])
                P.op('act', 'activation', out=YF.t[:, :, cs_], in_=ps[:, 0:256].rearrange("p (a b) -> p a b", a=4), func=AF.Copy,
                     r=[pk], w=YF.ks)
                P.op('dve', 'tensor_tensor', out=YF.t[64:128, :, cs_], in0=psO[64:128, 0:256].rearrange("p (a b) -> p a b", a=4),
                     in1=YF.t[64:128, :, cs_], op=ALU.add, r=[pkO] + YF.ks, w=YF.ks)
            ps, pk = P.next_ps()
            for h in range(8):
                hc, po, hs = hp(h)
                o_ = ps[po:po + 64, hc * 64:(hc + 1) * 64]
                P.op('pe', 'matmul', o_, lhsT=BT.t[:, hs], rhs=U.t[:, hs], start=True, stop=False, r=[BT.k, U.k], w=[pk])
                P.op('pe', 'matmul', o_, lhsT=KTk.t[:, hs], rhs=VT.t[:, hs], start=False, stop=True, r=[KTk.k, VT.k], w=[pk])
            wcb = WC.t[:, :, ci].unsqueeze(2).to_broadcast([128, 4, 64])
            P.op('pool', 'tensor_tensor', out=ST.t[:, :, :], in0=ST.t[:, :, :], in1=wcb, op=ALU.mult, r=[ST.k, WC.k], w=[ST.k])
            P.op('dve', 'tensor_tensor', out=ST.t[:, :, :], in0=ps[:, 0:256].rearrange("p (a b) -> p a b", a=4), in1=ST.t[:, :, :],
                 op=ALU.add, r=[pk, ST.k], w=[ST.k])
        if is_ctx:
            return None
        BON = IC
        for c in range(4):
            ps1, pk1 = P.next_ps()
            P.op('pe', 'matmul', ps1[:, 0:n], lhsT=bones64, rhs=YF.t[:, c, :], start=True, stop=True, r=[YF.ks[c], CST.k], w=[pk1])
            sq = C.tmpbuf('sq')
            P.op('act', 'activation', out=R(sq.t[:, 0:n]), in_=YF.t[:, c, :], func=AF.Square, r=[YF.ks[c]], w=[sq.k])
            ps2, pk2 = P.next_ps()
            P.op('pe', 'matmul', ps2[:, 0:n], lhsT=bones64, rhs=sq.t[:, 0:n], start=True, stop=True, r=[sq.k, CST.k], w=[pk2])
            mn = C.tmpbuf('t1', 512, 2)
            P.op('act', 'activation', out=mn.t[:, 0:n], in_=ps1[:, 0:n], func=AF.Copy, r=[pk1], w=[mn.k])
            m2 = C.tmpbuf('t2', 512, 2)
            P.op('pool', 'tensor_tensor', out=m2.t[:, 0:n], in0=mn.t[:, 0:n], in1=mn.t[:, 0:n], op=ALU.mult, r=[mn.k], w=[m2.k])
            P.op('dve', 'tensor_tensor', out=m2.t[:, 0:n], in0=ps2[:, 0:n], in1=m2.t[:, 0:n], op=ALU.subtract, r=[pk2, m2.k], w=[m2.k])
            sd = C.tmpbuf('sd', 512, 1)
            P.op('act', 'activation', out=sd.t[:, 0:n], in_=m2.t[:, 0:n], func=AF.Sqrt, bias=EPS.t[:, 1:2], scale=1.0,
                 r=[m2.k, EPS.k], w=[sd.k])
            rs = C.tmpbuf('rs', 512, 1)
            P.op('dve', 'reciprocal', out=rs.t[:, 0:n], in_=sd.t[:, 0:n], r=[sd.k], w=[rs.k])
            P.op('pool', 'tensor_tensor', out=YF.t[:, c, :], in0=YF.t[:, c, :], in1=mn.t[:, 0:n], op=ALU.subtract,
                 r=[YF.ks[c], mn.k], w=[YF.ks[c]])
            P.op('dve', 'tensor_tensor', out=YF.t[:, c, :], in0=YF.t[:, c, :], in1=rs.t[:, 0:n], op=ALU.mult,
                 r=[YF.ks[c], rs.k], w=[YF.ks[c]])
            P.op('act', 'activation', out=YF.t[:, c, :], in_=YF.t[:, c, :], func=AF.Identity, scale=own(6, c), bias=own(7, c),
                 r=[YF.ks[c], VEC.k], w=[YF.ks[c]])
            P.op('pool', 'tensor_tensor', out=YF.t[:, c, :], in0=YF.t[:, c, :], in1=BON.t[:, c, :], op=ALU.add,
                 r=[YF.ks[c], BON.ks[c]], w=[YF.ks[c]])
            P.op('dve', 'tensor_tensor', out=YF.t[:, c, :], in0=YF.t[:, c, :], in1=G.t[:, c, :], op=ALU.mult,
                 r=[YF.ks[c], G.ks[c]], w=[YF.ks[c]])
        ok = Trk()
        P.dma('sp', dst_v[:, :, d0:d0 + n], YF.t[:, :, :], r=YF.ks, w=[ok])
        return ok

    outs = []
    for d in range(2):
        P.op('pool', 'memset', ST.t[:, :, :], 0.0, w=[ST.k])
        segment(d, cv_, 0, 1, True, None, 0)
        sorder = list(range(NSEG)) if d == 0 else list(range(NSEG - 1, -1, -1))
        for si in sorder:
            outs.append(segment(d, xv, si * n, 0, False, ofv if d == 0 else obv, si * n))
    P.wait_all('sp', outs)
    P.finish()
    return nc


def consts_B():
    cst = np.zeros((128, 5, 128), np.float32)
    cst[:, 0, :] = 1.0 / D
    for h in range(2):
        cst[64 * h:64 * h + 64, 1, 64 * h:64 * h + 64] = 1.0 / 64
        cst[64 * h:64 * h + 64, 2, 64 * h:64 * h + 64] = 1.0
    cst[:, 3, :] = np.eye(128, dtype=np.float32)
    p = np.arange(64)[:, None]
    f = np.arange(64)[None, :]
    msk = np.zeros((64, 4, 64), np.float32)
    msk[:, 0, :] = p < f
    msk[:, 1, :] = p <= f
    msk[:, 2, :] = p > f
    msk[:, 3, :] = p >= f
    smk = np.ones((128, 4 * SEGB), np.float32)
    smk[:, 0::64] = 0.0
    return cst, msk, smk


def run_B(x1, xc1, c, c_ctx, ada_w1, ada_b1, n1g, p, hsplit=2):
    B, S, _ = x1.shape
    NSEG = S // SEGB
    nc = build_B(NSEG)
    cst, msk, smk = consts_B()
    in_maps = []
    adaw = np.ascontiguousarray(ada_w1[:, 0:2048])
    lw1 = np.ascontiguousarray(np.concatenate([p['w1'][0], p['w1'][1], p['a1'], p['g1'][0], p['g1'][1]], 1))
    for b in range(B):
        xpad = np.zeros((S + 2, D), np.float32)
        xpad[1:S + 1] = x1[b]
        cpad = np.zeros((SEGB + 2, D), np.float32)
        cpad[1:SEGB + 1] = xc1[b]
        xTb = np.ascontiguousarray(xpad.T)
        cTb = np.ascontiguousarray(cpad.T)
        for hh in range(hsplit):
            o0 = 512 * hh
            osl = slice(o0, o0 + 512)
            vecs = np.zeros((128, VB_N), np.float32)
            vecs[:, VB_C:VB_C + 8] = colvec(c[b])
            vecs[:, VB_CC:VB_CC + 8] = colvec(c_ctx)
            vecs[:, VB_ADAB:VB_ADAB + 16] = colvec(ada_b1[0:2048])
            vecs[:, VB_G1:VB_G1 + 8] = colvec(n1g)
            for m in range(6):
                vecs[:, VB_MU + 8 * m:VB_MU + 8 * m + 8] = colvec(p['mu'][m])
            ownv = [p['w0'][0], p['w0'][1], p['a0'], p['k_k'], p['k_a'], p['r_k'].reshape(-1), p['gn_g'], p['gn_b']]
            for s_, vv in enumerate(ownv):
                vecs[:, VB_OWN + 4 * s_:VB_OWN + 4 * s_ + 4] = colvec(vv[osl])
            in_maps.append({
                "xT": xTb, "cT": cTb, "adaw": adaw,
                "wr": np.ascontiguousarray(p['w_rkv'][0][:, osl]), "wk": np.ascontiguousarray(p['w_rkv'][1][:, osl]),
                "wv": np.ascontiguousarray(p['w_rkv'][2][:, osl]), "lw1": lw1,
                "w2o": np.ascontiguousarray(np.concatenate([p['w2'][0][:, osl], p['w2'][1][:, osl]], 0)),
                "a2o": np.ascontiguousarray(p['a2'][:, osl]),
                "g2o": np.ascontiguousarray(np.stack([p['g2'][0][:, osl], p['g2'][1][:, osl]], 1)),
                "vecs": vecs, "cst": cst, "msk": msk, "smk": smk})
    return nc, in_maps


VC_C, VC_CC, VC_ADAB, VC_G2, VC_N = 0, 8, 16, 48, 56


def build_C(NT):
    nc = bass.Bass("TRN2", target_bir_lowering=False)
    nc.dge_precook = False
    C = Ctx(nc)
    P = C.P
    TOK = NT * 512
    xT = C.dram("xT", [D, TOK], "ExternalInput")
    ofT = C.dram("ofT", [D, TOK], "ExternalInput")
    obT = C.dram("obT", [D, TOK], "ExternalInput")
    wo = C.dram("wo", [D, D], "ExternalInput")
    w1 = C.dram("w1", [D, 4096], "ExternalInput")
    w2 = C.dram("w2", [4096, D], "ExternalInput")
    adaw = C.dram("adaw", [D, 4096], "ExternalInput")
    vecs = C.dram("vecs", [128, VC_N], "ExternalInput")
    cst = C.dram("cst", [128, 128], "ExternalInput")
    xo = C.dram("xo", [D, TOK], "ExternalOutput")
    X = Buf(P, 'X', [128, 8, 512], 8)
    H = Buf(P, 'H', [128, 8, 512], 8)
    O2 = Buf(P, 'O2', [128, 8, 512], 8)
    HID = Buf(P, 'HID', [128, 8, 512], 8)
    VEC = Buf(P, 'VEC', [128, VC_N])
    CST = Buf(P, 'CST', [128, 128])
    SC = Buf(P, 'SC', [128, 8, 2])
    MOD = Buf(P, 'MOD', [128, 32, 2])
    DER = Buf(P, 'DER', [128, 8])
    EPS = Buf(P, 'EPS', [128, 1])
    P.dma('sp', VEC.t[:], vecs, w=[VEC.k])
    P.dma('sp', R(CST.t[:]), R(cst), w=[CST.k])
    P.op('pool', 'memset', EPS.t[:], NORM_EPS, w=[EPS.k])
    P.op('act', 'activation', out=SC.t[:, :, 0], in_=VEC.t[:, VC_C:VC_C + 8], func=AF.Silu, r=[VEC.k], w=[SC.k])
    P.op('act', 'activation', out=SC.t[:, :, 1], in_=VEC.t[:, VC_CC:VC_CC + 8], func=AF.Silu, r=[VEC.k], w=[SC.k])
    ada_vectors(C, adaw, MOD, SC, VEC.k, VEC.t[:, VC_ADAB:VC_ADAB + 32])
    P.op('dve', 'scalar_tensor_tensor', out=DER.t[:, :], in0=MOD.t[:, 16:24, 0], scalar=1.0, op0=ALU.add,
         in1=VEC.t[:, VC_G2:VC_G2 + 8], op1=ALU.mult, r=[MOD.k, VEC.k], w=[DER.k])
    GT1 = MOD.t[:, 0:8, 0]
    SH2 = MOD.t[:, 8:16, 0]
    GT2 = MOD.t[:, 24:32, 0]
    onesD = (CST.t[:, :], CST.k)
    epsap = (EPS.t[:, 0:1], EPS.k)
    xv = xT.rearrange("(k p) t -> p k t", p=128)
    fv = ofT.rearrange("(k p) t -> p k t", p=128)
    bv = obT.rearrange("(k p) t -> p k t", p=128)
    xov = xo.rearrange("(k p) t -> p k t", p=128)
    wov = wo.rearrange("(k p) c -> p k c", p=128)
    outs = []
    for i in range(NT):
        t0 = i * 512
        P.dma('sp', X.t[:, :, :], xv[:, :, t0:t0 + 512], w=X.ks)
        P.dma('sp', R(H.t[:, :, :]), R(fv[:, :, t0:t0 + 512]), w=H.ks)
        P.dma('sp', O2.t[:, :, :], bv[:, :, t0:t0 + 512], w=O2.ks)
        for k in range(KC):
            P.op('dve', 'tensor_tensor', out=R(H.t[:, k, :]), in0=H.t[:, k, :], in1=O2.t[:, k, :], op=ALU.add,
                 r=[H.ks[k], O2.ks[k]], w=[H.ks[k]])
        for nb in range(2):
            wv, wk = C.wload(wov[:, :, nb * 512:(nb + 1) * 512], [128, 8, 512])

            def evo(c, ps, pk, nb=nb):
                ch = nb * 4 + c
                P.op('dve', 'scalar_tensor_tensor', out=X.t[:, ch, :], in0=ps[:, :], scalar=GT1[:, ch:ch + 1], op0=ALU.mult,
                     in1=X.t[:, ch, :], op1=ALU.add, r=[pk, MOD.k, X.ks[ch]], w=[X.ks[ch]])
            linear_blk(C, wv, wk, 4, H.t, H.ks, 0, 512, evo)
        norm_mod(C, X.t, X.ks, 0, 512, H.t, H.ks, 0, DER.t[:, :], SH2, MOD.k, onesD, epsap)
        mlp(C, X.t, X.ks, 0, 512, H.t, H.ks, HID, w1, w2, GT2, MOD.k)
        ok = Trk()
        P.dma('sp', xov[:, :, t0:t0 + 512], X.t[:, :, :], r=X.ks, w=[ok])
        outs.append(ok)
    P.wait_all('sp', outs)
    P.finish()
    return nc


def run_C(x1, o_f, o_b, c, ada_w1, ada_b1, n2g, w_o, w1, w2, cores_per_batch):
    B, S, _ = x1.shape
    cpb = cores_per_batch
    TOK = S // cpb
    nc = build_C(TOK // 512)
    adaw = np.ascontiguousarray(ada_w1[:, 2048:6144])
    cst = np.full((128, 128), 1.0 / D, np.float32)
    in_maps = []
    for b in range(B):
        for hf in range(cpb):
            sl = slice(hf * TOK, (hf + 1) * TOK)
            vecs = np.zeros((128, VC_N), np.float32)
            vecs[:, VC_C:VC_C + 8] = colvec(c[b])
            vecs[:, VC_ADAB:VC_ADAB + 32] = colvec(ada_b1[2048:6144])
            vecs[:, VC_G2:VC_G2 + 8] = colvec(n2g)
            in_maps.append({"xT": np.ascontiguousarray(x1[b, sl].T), "ofT": np.ascontiguousarray(o_f[b, sl].T),
                            "obT": np.ascontiguousarray(o_b[b, sl].T), "wo": w_o, "w1": w1, "w2": w2, "adaw": adaw,
                            "vecs": vecs, "cst": cst})
    return nc, in_maps


def _launch(nc, in_maps):
    res = run_bass_kernel_spmd(nc, in_maps, core_ids=list(range(len(in_maps))))
    return res.results


def kernel(x, c, ctx, c_ctx, ada_w, ada_b, norm1_g, norm2_g, mlp_w1, mlp_w2,
           att_w_qkv, att_q_gain, att_k_gain, att_sink, att_w_o,
           rwkv_mu, rwkv_w_rkv, rwkv_w0, rwkv_w1, rwkv_w2, rwkv_a0, rwkv_a1, rwkv_a2,
           rwkv_g1, rwkv_g2, rwkv_k_k, rwkv_k_a, rwkv_r_k, rwkv_gn_g, rwkv_gn_b, rwkv_w_o):
    f = lambda a: np.ascontiguousarray(np.asarray(a, dtype=np.float32))
    x, c, ctx, c_ctx, ada_w, ada_b = f(x), f(c), f(ctx), f(c_ctx), f(ada_w), f(ada_b)
    norm1_g, norm2_g, mlp_w1, mlp_w2 = f(norm1_g), f(norm2_g), f(mlp_w1), f(mlp_w2)
    B, S, _ = x.shape
    cpb = 8 // B
    nc, maps = run_A(x, c, ctx, c_ctx, ada_w[0], ada_b[0], norm1_g[0], norm2_g[0], f(att_w_qkv)[0], f(att_q_gain)[0],
                     f(att_k_gain)[0], f(att_sink)[0], f(att_w_o)[0], mlp_w1[0], mlp_w2[0], cpb)
    res = _launch(nc, maps)
    TOK = S // cpb
    x1 = np.empty((B, S, D), np.float32)
    xc1 = np.empty((B, ctx.shape[1], D), np.float32)
    for b in range(B):
        for hf in range(cpb):
            x1[b, hf * TOK:(hf + 1) * TOK] = res[b * cpb + hf]["xo"].T
        xc1[b] = res[b * cpb]["co"].T
    p = dict(mu=f(rwkv_mu)[0], w_rkv=f(rwkv_w_rkv)[0], w0=f(rwkv_w0)[0], w1=f(rwkv_w1)[0], w2=f(rwkv_w2)[0], a0=f(rwkv_a0)[0],
             a1=f(rwkv_a1)[0], a2=f(rwkv_a2)[0], g1=f(rwkv_g1)[0], g2=f(rwkv_g2)[0], k_k=f(rwkv_k_k)[0], k_a=f(rwkv_k_a)[0],
             r_k=f(rwkv_r_k)[0], gn_g=f(rwkv_gn_g)[0], gn_b=f(rwkv_gn_b)[0])
    nc, maps = run_B(x1, xc1, c, c_ctx, ada_w[1], ada_b[1], norm1_g[1], p, hsplit=cpb)
    res = _launch(nc, maps)
    o_f = np.empty((B, S, D), np.float32)
    o_b = np.empty((B, S, D), np.float32)
    hw = D // cpb
    for b in range(B):
        for hh in range(cpb):
            o_f[b, :, hh * hw:(hh + 1) * hw] = res[b * cpb + hh]["of"].T
            o_b[b, :, hh * hw:(hh + 1) * hw] = res[b * cpb + hh]["ob"].T
    nc, maps = run_C(x1, o_f, o_b, c, ada_w[1], ada_b[1], norm2_g[1], f(rwkv_w_o)[0], mlp_w1[1], mlp_w2[1], cpb)
    res = _launch(nc, maps)
    out = np.empty((B, S, D), np.float32)
    for b in range(B):
        for hf in range(cpb):
            out[b, hf * TOK:(hf + 1) * TOK] = res[b * cpb + hf]["xo"].T
    return out
```
